# Optimizing a Trainium2 kernel written in Bass

```python
import math
import jax
import jax.numpy as jnp
from jax import lax
import numpy as np

D_MODEL = 1024
BATCH = 2
SEQ = 16384
DEPTH = 4

CTX_LEN = 256
GRID_W = 64

RET_HEADS = 4
RET_DK = 128
RET_DV = 256
RET_CHUNK = 128
RET_QK = RET_HEADS * RET_DK
RET_V = RET_HEADS * RET_DV

DN_HEADS = 4
DN_DK = 128
DN_DV = 256
DN_CHUNK = 64
DN_CONV = 5
DN_QK = DN_HEADS * DN_DK
DN_V = DN_HEADS * DN_DV
DN_CONV_CH = 2 * DN_QK + DN_V

FFN_HIDDEN = 2816

ROPE_BASE = 10000.0
EPS = 1e-6
N_MOD = 9

IN_SIZES = (RET_QK, RET_QK, RET_V, RET_V, DN_QK, DN_QK, DN_V, DN_V, 4 * DN_HEADS, D_MODEL, D_MODEL)
IN_WIDTH = sum(IN_SIZES)

kernel_name = "hybrid_retention_deltanet_prefix_dit"


def _split_points(sizes):
    pts, acc = [], 0
    for s in sizes[:-1]:
        acc += s
        pts.append(acc)
    return pts


def rmsnorm(x, w):
    xf = x.astype(jnp.float32)
    y = xf * lax.rsqrt(jnp.mean(xf * xf, axis=-1, keepdims=True) + EPS)
    return (y * w.astype(jnp.float32)).astype(x.dtype)


def modulate(h, shift, scale):
    return h * (1.0 + scale) + shift


def swiglu(h, w_gu, w_down):
    gate, up = jnp.split(h @ w_gu, 2, axis=-1)
    return (jax.nn.silu(gate) * up) @ w_down


def _heads(t, n_heads):
    b, l, _ = t.shape
    return t.reshape(b, l, n_heads, -1).transpose(0, 2, 1, 3).astype(jnp.float32)


def _merge_heads(t):
    b, h, l, d = t.shape
    return t.transpose(0, 2, 1, 3).reshape(b, l, h * d)


def _flip(t):
    return jnp.flip(t, axis=2)


def l2norm(t):
    return t * lax.rsqrt(jnp.sum(t * t, axis=-1, keepdims=True) + EPS)


def axial_rope(rows):
    half = RET_DK // 2
    n_freq = half // 2
    inv = ROPE_BASE ** (-jnp.arange(n_freq, dtype=jnp.float32) / n_freq)
    ang_r = jnp.arange(rows, dtype=jnp.float32)[:, None] * inv
    ang_c = jnp.arange(GRID_W, dtype=jnp.float32)[:, None] * inv
    ang = jnp.concatenate([
        jnp.broadcast_to(ang_r[:, None, :], (rows, GRID_W, n_freq)),
        jnp.broadcast_to(ang_c[None, :, :], (rows, GRID_W, n_freq)),
    ], axis=-1).reshape(rows * GRID_W, half)
    return jnp.cos(ang), jnp.sin(ang)


def apply_rope(t, cos, sin):
    half = t.shape[-1] // 2
    t1, t2 = t[..., :half], t[..., half:]
    return jnp.concatenate([t1 * cos - t2 * sin, t2 * cos + t1 * sin], axis=-1)


def short_conv(u, w):
    pad = DN_CONV // 2
    y = lax.conv_general_dilated(
        u, w[:, None, :], window_strides=(1,), padding=[(pad, pad)],
        dimension_numbers=("NWC", "WIO", "NWC"), feature_group_count=u.shape[-1])
    return jax.nn.silu(y)


def retention_chunked(q, k, v, log_gamma, s0, include_diag):
    b, h, l, dk = q.shape
    dv = v.shape[-1]
    c = RET_CHUNK
    n = l // c
    qc = q.reshape(b, h, n, c, dk)
    kc = k.reshape(b, h, n, c, dk)
    vc = v.reshape(b, h, n, c, dv)
    idx = jnp.arange(c, dtype=jnp.float32)
    diff = idx[:, None] - idx[None, :]
    mask = diff >= 0 if include_diag else diff > 0
    decay = jnp.where(mask, jnp.exp(log_gamma[:, None, None] * jnp.where(mask, diff, 0.0)), 0.0)
    scores = jnp.einsum("bhncd,bhnmd->bhncm", qc, kc) * decay[None, :, None]
    o_intra = jnp.einsum("bhncm,bhnme->bhnce", scores, vc)
    k_dec = kc * jnp.exp(log_gamma[:, None] * (c - 1.0 - idx)[None, :])[None, :, None, :, None]
    chunk_kv = jnp.einsum("bhncd,bhnce->bhnde", k_dec, vc)
    gamma_c = jnp.exp(log_gamma * c)[None, :, None, None]

    def step(s, kv):
        return s * gamma_c + kv, s

    s_fin, s_prev = lax.scan(step, s0, jnp.moveaxis(chunk_kv, 2, 0))
    s_prev = jnp.moveaxis(s_prev, 0, 2)
    q_dec = qc * jnp.exp(log_gamma[:, None] * (idx + 1.0)[None, :])[None, :, None, :, None]
    o_inter = jnp.einsum("bhncd,bhnde->bhnce", q_dec, s_prev)
    return (o_intra + o_inter).reshape(b, h, l, dv), s_fin


def gated_delta_chunked(q, k, v, g, beta, s0):
    b, h, l, dk = q.shape
    dv = v.shape[-1]
    c = DN_CHUNK
    n = l // c
    q = q.reshape(b, h, n, c, dk)
    k = k.reshape(b, h, n, c, dk)
    v = v.reshape(b, h, n, c, dv)
    g = g.reshape(b, h, n, c)
    beta = beta.reshape(b, h, n, c)
    gc = jnp.cumsum(g, axis=-1)
    incl = jnp.tril(jnp.ones((c, c), dtype=bool))
    strict = jnp.tril(jnp.ones((c, c), dtype=bool), -1)
    gdiff = gc[..., :, None] - gc[..., None, :]
    decay = jnp.where(incl, jnp.exp(jnp.where(incl, gdiff, 0.0)), 0.0)
    kb = k * beta[..., None]
    a = jnp.where(strict, jnp.einsum("bhnid,bhnjd->bhnij", kb, k) * decay, 0.0)
    eye = jnp.eye(c, dtype=jnp.float32)
    t = lax.linalg.triangular_solve(eye + a, jnp.broadcast_to(eye, a.shape),
                                    left_side=True, lower=True, unit_diagonal=True)
    u = jnp.einsum("bhnij,bhnje->bhnie", t, v * beta[..., None])
    w = jnp.einsum("bhnij,bhnjd->bhnid", t, kb * jnp.exp(gc)[..., None])
    attn = jnp.where(incl, jnp.einsum("bhnid,bhnjd->bhnij", q, k) * decay, 0.0)
    qg = q * jnp.exp(gc)[..., None]
    glast = gc[..., -1]
    kd = k * jnp.exp(glast[..., None] - gc)[..., None]

    def step(s, xs):
        u_n, w_n, qg_n, attn_n, kd_n, gl_n = xs
        v_new = u_n - jnp.einsum("bhcd,bhde->bhce", w_n, s)
        o = jnp.einsum("bhcd,bhde->bhce", qg_n, s) + jnp.einsum("bhij,bhje->bhie", attn_n, v_new)
        s = s * jnp.exp(gl_n)[..., None, None] + jnp.einsum("bhcd,bhce->bhde", kd_n, v_new)
        return s, o

    xs = tuple(jnp.moveaxis(z, 2, 0) for z in (u, w, qg, attn, kd, glast))
    s_fin, o = lax.scan(step, s0, xs)
    return jnp.moveaxis(o, 0, 2).reshape(b, h, l, dv), s_fin


def mixer_features(p, cos_sin, conv_w, a_log, dt_bias):
    b, l, _ = p.shape
    (rq, rk, rv, rg, dn_q, dn_k, dn_v, dz, dba, gate_a, gate_b) = jnp.split(p, _split_points(IN_SIZES), axis=-1)
    rq = _heads(rq, RET_HEADS)
    rk = _heads(rk, RET_HEADS) * (RET_DK ** -0.5)
    if cos_sin is not None:
        cos, sin = cos_sin
        rq = apply_rope(rq, cos, sin)
        rk = apply_rope(rk, cos, sin)
    rv = _heads(rv, RET_HEADS)
    qkv = short_conv(jnp.concatenate([dn_q, dn_k, dn_v], axis=-1).astype(jnp.float32), conv_w.astype(jnp.float32))
    dn_q, dn_k, dn_v = jnp.split(qkv, [DN_QK, 2 * DN_QK], axis=-1)
    dn_q = l2norm(_heads(dn_q, DN_HEADS)) * (DN_DK ** -0.5)
    dn_k = l2norm(_heads(dn_k, DN_HEADS))
    dn_v = _heads(dn_v, DN_HEADS)
    ba = dba.astype(jnp.float32).reshape(b, l, 4, DN_HEADS).transpose(2, 0, 3, 1)
    beta = jax.nn.sigmoid(ba[:2])
    g = -jnp.exp(a_log.astype(jnp.float32))[:, None, :, None] * jax.nn.softplus(
        ba[2:] + dt_bias.astype(jnp.float32)[:, None, :, None])
    return (rq, rk, rv, dn_q, dn_k, dn_v, beta, g), (rg, dz, gate_a, gate_b)


def bidirectional_mix(fc, fx, log_gamma):
    crq, crk, crv, cdq, cdk, cdv, cbeta, cg = fc
    xrq, xrk, xrv, xdq, xdk, xdv, xbeta, xg = fx
    b = crq.shape[0]
    s_ret0 = jnp.zeros((b, RET_HEADS, RET_DK, RET_DV), jnp.float32)
    s_dn0 = jnp.zeros((b, DN_HEADS, DN_DK, DN_DV), jnp.float32)
    f = _flip
    rc_f, sr_f = retention_chunked(crq, crk, crv, log_gamma, s_ret0, True)
    rc_b, sr_b = retention_chunked(f(crq), f(crk), f(crv), log_gamma, s_ret0, False)
    rx_f, _ = retention_chunked(xrq, xrk, xrv, log_gamma, sr_f, True)
    rx_b, _ = retention_chunked(f(xrq), f(xrk), f(xrv), log_gamma, sr_b, False)
    dc_f, sd_f = gated_delta_chunked(cdq, cdk, cdv, cg[0], cbeta[0], s_dn0)
    dc_b, sd_b = gated_delta_chunked(f(cdq), f(cdk), f(cdv), f(cg[1]), f(cbeta[1]), s_dn0)
    dx_f, _ = gated_delta_chunked(xdq, xdk, xdv, xg[0], xbeta[0], sd_f)
    dx_b, _ = gated_delta_chunked(f(xdq), f(xdk), f(xdv), f(xg[1]), f(xbeta[1]), sd_b)
    ctx_out = (rc_f + f(rc_b), dc_f + f(dc_b))
    lat_out = (rx_f + f(rx_b), dx_f + f(dx_b))
    return ctx_out, lat_out


def mixer_output(ret_o, dn_o, gates, dn_norm_w, w_ret_out, w_dn_out, w_o, dtype):
    rg, dz, gate_a, gate_b = gates
    ret = ret_o * lax.rsqrt(jnp.mean(ret_o * ret_o, axis=-1, keepdims=True) + EPS)
    ret = _merge_heads(ret).astype(dtype) * jax.nn.silu(rg)
    dn = dn_o * lax.rsqrt(jnp.mean(dn_o * dn_o, axis=-1, keepdims=True) + EPS) * dn_norm_w.astype(jnp.float32)
    dn = _merge_heads(dn).astype(dtype) * jax.nn.silu(dz)
    y = jax.nn.sigmoid(gate_a) * (ret @ w_ret_out) + jax.nn.sigmoid(gate_b) * (dn @ w_dn_out)
    return y @ w_o


def setup_inputs(seed: int = 0) -> dict:
    key = jax.random.key(seed)
    ks = jax.random.split(key, 24)
    f32 = jnp.float32

    def nrm(k, shape, fan_in, gain=1.0):
        return jax.random.normal(k, shape, f32) * (gain * fan_in ** -0.5)

    a_init = jax.random.uniform(ks[10], (DEPTH, 2, DN_HEADS), f32, 1.0, 16.0)
    dt = jnp.exp(jax.random.uniform(ks[11], (DEPTH, 2, DN_HEADS), f32, math.log(1e-3), math.log(1e-1)))
    dt_bias = dt + jnp.log(-jnp.expm1(-dt))
    return {
        "x": jax.random.normal(ks[0], (BATCH, SEQ, D_MODEL), f32),
        "c": jax.random.normal(ks[1], (BATCH, D_MODEL), f32),
        "ctx": jax.random.normal(ks[2], (BATCH, CTX_LEN, D_MODEL), f32),
        "c_ctx": jax.random.normal(ks[3], (D_MODEL,), f32),
        "ada_w": nrm(ks[4], (DEPTH, D_MODEL, N_MOD * D_MODEL), D_MODEL, 0.5),
        "ada_b": 0.02 * jax.random.normal(ks[5], (DEPTH, N_MOD * D_MODEL), f32),
        "norm_w": 1.0 + 0.05 * jax.random.normal(ks[6], (DEPTH, 3, D_MODEL), f32),
        "ffn1_wgu": nrm(ks[7], (DEPTH, D_MODEL, 2 * FFN_HIDDEN), D_MODEL),
        "ffn1_wd": nrm(ks[8], (DEPTH, FFN_HIDDEN, D_MODEL), FFN_HIDDEN),
        "w_in": nrm(ks[9], (DEPTH, D_MODEL, IN_WIDTH), D_MODEL),
        "dn_conv_w": nrm(ks[12], (DEPTH, DN_CONV, DN_CONV_CH), DN_CONV),
        "dn_a_log": jnp.log(a_init),
        "dn_dt_bias": dt_bias,
        "dn_norm_w": 1.0 + 0.05 * jax.random.normal(ks[13], (DEPTH, DN_DV), f32),
        "w_ret_out": nrm(ks[14], (DEPTH, RET_V, D_MODEL), RET_V),
        "w_dn_out": nrm(ks[15], (DEPTH, DN_V, D_MODEL), DN_V),
        "w_o": nrm(ks[16], (DEPTH, D_MODEL, D_MODEL), D_MODEL),
        "ffn2_wgu": nrm(ks[17], (DEPTH, D_MODEL, 2 * FFN_HIDDEN), D_MODEL),
        "ffn2_wd": nrm(ks[18], (DEPTH, FFN_HIDDEN, D_MODEL), FFN_HIDDEN),
        "final_norm_w": 1.0 + 0.05 * jax.random.normal(ks[19], (D_MODEL,), f32),
    }


def reference(x, c, ctx, c_ctx, ada_w, ada_b, norm_w, ffn1_wgu, ffn1_wd, w_in, dn_conv_w,
              dn_a_log, dn_dt_bias, dn_norm_w, w_ret_out, w_dn_out, w_o, ffn2_wgu, ffn2_wd,
              final_norm_w):
    b, l, _ = x.shape
    rows = l // GRID_W
    rope = axial_rope(rows)
    log_gamma = jnp.log1p(-jnp.power(2.0, -5.0 - jnp.arange(RET_HEADS, dtype=jnp.float32)))
    silu_c = jax.nn.silu(c)
    silu_cc = jax.nn.silu(c_ctx)[None, :]
    cx = ctx
    for layer in range(DEPTH):
        last = layer == DEPTH - 1
        mx = jnp.split((silu_c @ ada_w[layer] + ada_b[layer])[:, None, :], N_MOD, axis=-1)
        mc = jnp.split((silu_cc @ ada_w[layer] + ada_b[layer])[:, None, :], N_MOD, axis=-1)

        x = x + 0.5 * mx[2] * swiglu(modulate(rmsnorm(x, norm_w[layer, 0]), mx[0], mx[1]),
                                     ffn1_wgu[layer], ffn1_wd[layer])
        cx = cx + 0.5 * mc[2] * swiglu(modulate(rmsnorm(cx, norm_w[layer, 0]), mc[0], mc[1]),
                                       ffn1_wgu[layer], ffn1_wd[layer])

        hx = modulate(rmsnorm(x, norm_w[layer, 1]), mx[3], mx[4])
        hc = modulate(rmsnorm(cx, norm_w[layer, 1]), mc[3], mc[4])
        fx, gx = mixer_features(hx @ w_in[layer], rope, dn_conv_w[layer], dn_a_log[layer], dn_dt_bias[layer])
        fc, gc = mixer_features(hc @ w_in[layer], None, dn_conv_w[layer], dn_a_log[layer], dn_dt_bias[layer])
        (ret_c, dn_c), (ret_x, dn_x) = bidirectional_mix(fc, fx, log_gamma)
        x = x + mx[5] * mixer_output(ret_x, dn_x, gx, dn_norm_w[layer], w_ret_out[layer],
                                     w_dn_out[layer], w_o[layer], x.dtype)

        x = x + 0.5 * mx[8] * swiglu(modulate(rmsnorm(x, norm_w[layer, 2]), mx[6], mx[7]),
                                     ffn2_wgu[layer], ffn2_wd[layer])
        if not last:
            cx = cx + mc[5] * mixer_output(ret_c, dn_c, gc, dn_norm_w[layer], w_ret_out[layer],
                                           w_dn_out[layer], w_o[layer], cx.dtype)
            cx = cx + 0.5 * mc[8] * swiglu(modulate(rmsnorm(cx, norm_w[layer, 2]), mc[6], mc[7]),
                                           ffn2_wgu[layer], ffn2_wd[layer])
    return rmsnorm(x, final_norm_w)
```

```python
import numpy as np
from contextlib import ExitStack
import concourse.bass as bass
import concourse.mybir as mybir
from concourse.bass_utils import run_bass_kernel_spmd

F32 = mybir.dt.float32
BF16 = mybir.dt.bfloat16
AF = mybir.ActivationFunctionType
ALU = mybir.AluOpType

D = 1024
KD = 8
HID = 2816
KH = 22
NMOD = 9
EPS = 1e-6
LAT = 16384
CTX = 256
NCORES = 8
LAT_PC = LAT * 2 // NCORES
CTX_PC = CTX * 2 // NCORES
TOK = LAT_PC + CTX_PC


class Res:
    __slots__ = ("name", "last_w", "readers")

    def __init__(self, name):
        self.name = name
        self.last_w = None
        self.readers = []


class Tl:
    def __init__(self, t, name):
        self.t = t
        self.res = Res(name)
        self.subs = {}

    def sub(self, key):
        r = self.subs.get(key)
        if r is None:
            r = Res("%s/%s" % (self.res.name, key))
            r.last_w = self.res.last_w
            r.readers = list(self.res.readers)
            self.subs[key] = r
        return r

    def __getitem__(self, idx):
        return self.t[idx]


ENGS = ["pe", "act", "dve", "pool", "sp"]


class Prog:
    def __init__(self, nc):
        self.nc = nc
        self.es = ExitStack()
        self.ops = {e: [] for e in ENGS}
        self.dsem = {}
        self.nsem = 0
        self.uid = 0

    def sb(self, name, shape, dt):
        self.uid += 1
        t = self.es.enter_context(self.nc.sbuf_tensor("%s_%d" % (name, self.uid), list(shape), dt))
        return Tl(t, name)

    def ps(self, name, shape, dt=F32):
        self.uid += 1
        t = self.es.enter_context(self.nc.psum_tensor("%s_%d" % (name, self.uid), list(shape), dt))
        return Tl(t, name)

    def _res(self, lst):
        out = []
        for r in lst:
            if isinstance(r, Tl):
                out.append(r.res)
                out.extend(r.subs.values())
            elif isinstance(r, Res):
                out.append(r)
            elif r is None:
                pass
            else:
                raise TypeError(r)
        return out

    def _deps(self, reads, writes):
        deps = []
        for r in reads:
            if r.last_w is not None:
                deps.append(r.last_w)
        for w in writes:
            if w.last_w is not None:
                deps.append(w.last_w)
            deps.extend(w.readers)
        return deps

    def _commit(self, tok, reads, writes):
        for r in reads:
            r.readers.append(tok)
        for w in writes:
            w.last_w = tok
            w.readers = []

    def op(self, eng, fn, reads=(), writes=()):
        reads = self._res(reads)
        writes = self._res(writes)
        deps = self._deps(reads, writes)
        idx = len(self.ops[eng])
        self.ops[eng].append({"fn": fn, "deps": deps, "sig": False, "dma": None})
        tok = ("c", eng, idx)
        self._commit(tok, reads, writes)
        return tok

    def dma(self, q, out, in_, key, reads=(), writes=()):
        reads = self._res(reads)
        writes = self._res(writes)
        deps = self._deps(reads, writes)
        kname = id(key)
        if kname not in self.dsem:
            self.nsem += 1
            sem = self.es.enter_context(self.nc.semaphore("d%d" % self.nsem))
            self.dsem[kname] = [sem, 0]
        ent = self.dsem[kname]
        ent[1] += 1
        self.ops[q].append({"fn": lambda e, o=out, i=in_: e.dma_start(out=o, in_=i), "deps": deps,
                            "sig": False, "dma": ent[0]})
        tok = ("d", kname, ent[1])
        self._commit(tok, reads, writes)
        return tok

    def wait_all_dma(self, eng, keys):
        deps = []
        for k in keys:
            ent = self.dsem[id(k)]
            deps.append(("d", id(k), ent[1]))
        self.ops[eng].append({"fn": None, "deps": deps, "sig": False, "dma": None})

    def emit(self):
        nc = self.nc
        for e in ENGS:
            for o in self.ops[e]:
                for d in o["deps"]:
                    if d[0] == "c":
                        if d[1] == "pe" and e == "pe":
                            continue
                        self.ops[d[1]][d[2]]["sig"] = True
        rank = {}
        esem = {}
        for e in ENGS:
            r = 0
            rk = []
            for o in self.ops[e]:
                if o["sig"]:
                    r += 1
                rk.append(r)
            rank[e] = rk
            esem[e] = self.es.enter_context(nc.semaphore("e_" + e))
        dsem_by = {k: v[0] for k, v in self.dsem.items()}

        def run(e, eng):
            waited = {}
            for o in self.ops[e]:
                need = {}
                for d in o["deps"]:
                    if d[0] == "c":
                        if d[1] == "pe" and e == "pe":
                            continue
                        s = esem[d[1]]
                        v = rank[d[1]][d[2]]
                        kk = ("c", d[1])
                    else:
                        s = dsem_by[d[1]]
                        v = 16 * d[2]
                        kk = ("d", d[1])
                    if waited.get(kk, 0) >= v:
                        continue
                    if kk not in need or need[kk][1] < v:
                        need[kk] = (s, v)
                for kk, (s, v) in need.items():
                    eng.wait_ge(s, v)
                    waited[kk] = v
                if o["fn"] is None:
                    continue
                ins = o["fn"](eng)
                if o["dma"] is not None:
                    ins.then_inc(o["dma"], 16)
                elif o["sig"]:
                    ins.then_inc(esem[e], 1)

        with nc.Block() as block:
            @block.tensor
            def _(eng):
                run("pe", eng)

            @block.scalar
            def _(eng):
                run("act", eng)

            @block.vector
            def _(eng):
                run("dve", eng)

            @block.gpsimd
            def _(eng):
                run("pool", eng)

            @block.sync
            def _(eng):
                run("sp", eng)
        self.es.close()

    def mm(self, out, lhsT, rhs, start, stop, reads, writes):
        return self.op("pe", lambda e: e.matmul(out, lhsT, rhs, start=start, stop=stop), reads, writes)

    def act(self, out, in_, func, reads, writes, bias=None, scale=None, eng="act"):
        kw = {}
        if bias is not None:
            kw["bias"] = bias
        if scale is not None:
            kw["scale"] = scale
        return self.op(eng, lambda e: e.activation(out, in_, func, **kw), reads, writes)

    def tt(self, eng, out, in0, in1, op, reads, writes):
        return self.op(eng, lambda e: e.tensor_tensor(out, in0, in1, op), reads, writes)

    def ts(self, eng, out, in0, s1, s2, op0, op1, reads, writes):
        if op1 is None:
            return self.op(eng, lambda e: e.tensor_scalar(out, in0, s1, None, op0), reads, writes)
        return self.op(eng, lambda e: e.tensor_scalar(out, in0, s1, s2, op0, op1), reads, writes)

    def stt(self, out, in0, scalar, in1, op0, op1, reads, writes):
        return self.op("dve", lambda e: e.scalar_tensor_tensor(out, in0, scalar, in1, op0, op1), reads, writes)

    def copy(self, eng, out, in_, reads, writes):
        if eng == "act":
            return self.op("act", lambda e: e.activation(out, in_, AF.Copy), reads, writes)
        return self.op(eng, lambda e: e.tensor_copy(out, in_), reads, writes)

    def memset(self, eng, ap, val, writes):
        return self.op(eng, lambda e: e.memset(ap, val), (), writes)


def col_layout(v):
    v = np.asarray(v, np.float32)
    return np.ascontiguousarray(v.reshape(-1, 128).T)


def emit_mod(P, dr, pref, consts, eps_col):
    nc = P.nc
    adaw = dr[pref + "ada_w"]
    adab = dr[pref + "ada_b"]
    nw = dr[pref + "norm_w"]
    sc = consts["sc"]
    modps = consts["modps"]
    NJ = 2
    wst = consts["adaw_st"]
    adv = adaw.rearrange("(k p) n -> p k n", p=128)
    for g in range(72 // NJ):
        w = wst[g % 2]
        P.dma("sp", w[:, :, :], adv[:, :, g * NJ * 128:(g + 1) * NJ * 128], key=w, writes=[w])
        for jj in range(NJ):
            j = g * NJ + jj
            for k in range(8):
                P.mm(modps[:, j, :], w[:, k, jj * 128:(jj + 1) * 128], sc[:, k, :], k == 0, k == 7,
                     reads=[w, sc], writes=[modps])
    adab_t = P.sb("adab", [128, 72], F32)
    nw_t = P.sb("nw", [128, 24], F32)
    P.dma("sp", adab_t[:, :], adab[:, :], key=adab_t, writes=[adab_t])
    P.dma("sp", nw_t[:, :], nw[:, :], key=nw_t, writes=[nw_t])
    mod = P.sb("mod", [128, 2, 72], F32)
    for s in range(2):
        P.tt("dve", mod[:, s, :], modps[:, :, s], adab_t[:, :], ALU.add, reads=[modps, adab_t], writes=[mod])
    acol = P.sb("acol", [128, 2, 24], F32)
    gcol = P.sb("gcol", [128, 2, 24], F32)
    for s in range(2):
        for i in range(3):
            P.stt(acol[:, s, i * 8:(i + 1) * 8], mod[:, s, (3 * i + 1) * 8:(3 * i + 2) * 8], 1.0,
                  nw_t[:, i * 8:(i + 1) * 8], ALU.add, ALU.mult, reads=[mod, nw_t], writes=[acol])
            P.ts("dve", gcol[:, s, i * 8:(i + 1) * 8], mod[:, s, (3 * i + 2) * 8:(3 * i + 3) * 8],
                 0.5 if i != 1 else 1.0, None, ALU.mult, None, reads=[mod], writes=[gcol])
    return {"mod": mod, "a": acol, "g": gcol}


def emit_norm(P, xt, N, md, s, i, hout, consts, tmp32, ssps, sq):
    ones = consts["ones_bf"]
    P.act(sq[:, :, :N], xt[:, :, :N], AF.Square, reads=[xt], writes=[sq])
    for k in range(8):
        P.mm(ssps[:, :N], ones[:, :], sq[:, k, :N], k == 0, k == 7, reads=[sq, ones], writes=[ssps])
    rstd = tmp32[0]
    P.act(rstd[:, :N], ssps[:, :N], AF.Sqrt, reads=[ssps, consts["eps"]], writes=[rstd],
          bias=consts["eps"][:, 0:1], scale=1.0 / D)
    P.op("dve", lambda e: e.reciprocal(rstd[:, :N], rstd[:, :N]), reads=[rstd], writes=[rstd])
    a = md["a"]
    mod = md["mod"]
    for k in range(8):
        t = tmp32[1 + (k % 2)]
        P.stt(t[:, :N], xt[:, k, :N], a[:, s, i * 8 + k:i * 8 + k + 1], rstd[:, :N], ALU.mult, ALU.mult,
              reads=[xt, a, rstd], writes=[t])
        P.act(hout[:, k, :N], t[:, :N], AF.Identity, reads=[t, mod], writes=[hout.sub(k)],
              bias=mod[:, s, (3 * i) * 8 + k:(3 * i) * 8 + k + 1])


def load_w_bf16(P, dst, src_ap, nk, ncols, piece=1024):
    v = src_ap.rearrange("(k p) n -> p k n", p=128)
    for k in range(nk):
        for c0 in range(0, ncols, piece):
            c1 = min(ncols, c0 + piece)
            P.dma("pool", dst[:, k, c0:c1], v[:, k, c0:c1], key=dst, writes=[dst])


def tiles_of(core_tok_lat, core_tok_ctx, TT):
    tl = []
    for t0 in range(0, core_tok_lat, TT):
        tl.append((t0, TT, 0))
    if core_tok_ctx:
        tl.append((core_tok_lat, core_tok_ctx, 1))
    return tl


def dres(bufs, ap, ti):
    k = ("dram", id(ap), ti)
    if k not in bufs:
        bufs[k] = Res("dram")
    return bufs[k]


def emit_ffn_pass(P, dr, consts, md, i_norm, wgu_name, wd_name, src, dst, tiles, TT, bufs,
                  post=None):
    wgu = bufs["wbig"]
    wd = bufs["wd"]
    load_w_bf16(P, wgu, dr[wgu_name], 8, 2 * HID)
    load_w_bf16(P, wd, dr[wd_name], KH, D)
    srcv = src.rearrange("(k p) t -> p k t", p=128)
    dstv = dst.rearrange("(k p) t -> p k t", p=128)
    g = md["g"]

    def load(ti):
        t0, N, s = tiles[ti]
        xt = bufs["x"][ti % 2]
        P.dma("sp", xt[:, :, :N], srcv[:, :, t0:t0 + N], key=xt, reads=[dres(bufs, src, ti)], writes=[xt])

    load(0)
    for ti, (t0, N, s) in enumerate(tiles):
        if ti + 1 < len(tiles):
            load(ti + 1)
        xt = bufs["x"][ti % 2]
        h = bufs["h"][ti % 2]
        actb = bufs["act"]
        emit_norm(P, xt, N, md, s, i_norm, h, consts, bufs["tmp32"], bufs["ssps"], bufs["sq"])
        for j in range(KH):
            gp = bufs["gps"][j % 2]
            up = bufs["ups"][j % 2]
            for k in range(8):
                P.mm(gp[:, :N], wgu[:, k, j * 128:(j + 1) * 128], h[:, k, :N], k == 0, k == 7,
                     reads=[wgu, h.sub(k)], writes=[gp])
            for k in range(8):
                P.mm(up[:, :N], wgu[:, k, HID + j * 128:HID + (j + 1) * 128], h[:, k, :N], k == 0, k == 7,
                     reads=[wgu, h.sub(k)], writes=[up])
            sg = bufs["sg"][j % 2]
            P.act(sg[:, :N], gp[:, :N], AF.Silu, reads=[gp], writes=[sg])
            P.tt("dve", actb[:, j, :N], sg[:, :N], up[:, :N], ALU.mult, reads=[sg, up], writes=[actb.sub(j)])
        for d in range(8):
            yp = bufs["yps"][d % 2]
            for j in range(KH):
                P.mm(yp[:, :N], wd[:, j, d * 128:(d + 1) * 128], actb[:, j, :N], j == 0, j == KH - 1,
                     reads=[wd, actb.sub(j)], writes=[yp])
            P.stt(xt[:, d, :N], yp[:, :N], g[:, s, i_norm * 8 + d:i_norm * 8 + d + 1], xt[:, d, :N],
                  ALU.mult, ALU.add, reads=[yp, g, xt], writes=[xt])
        if post is not None:
            post(xt, N, s, t0, ti)
        P.dma("sp", dstv[:, :, t0:t0 + N], xt[:, :, :N], key=bufs["st_" + str(ti % 2)], reads=[xt],
              writes=[dres(bufs, dst, ti)])


def build_rowlocal(do_mix, do_ffn2, do_ffn1, do_final, with_ctx=True, TT=256):
    nc = bass.Bass("TRN2", target_bir_lowering=False)
    P = Prog(nc)
    dr = {}

    def din(name, shape, dt=F32):
        dr[name] = nc.dram_tensor(name, list(shape), dt, kind="ExternalInput").ap()

    def dout(name, shape, dt=F32):
        dr[name] = nc.dram_tensor(name, list(shape), dt, kind="ExternalOutput").ap()

    ntok = TOK if with_ctx else LAT_PC
    din("x_in", [D, ntok])
    din("ccol", [128, 8, 2])
    if do_mix or do_ffn2:
        for n, shp in (("A_ada_w", [D, NMOD * D]), ("A_ada_b", [128, 72]), ("A_norm_w", [128, 24])):
            din(n, shp)
    if do_ffn1:
        for n, shp in (("B_ada_w", [D, NMOD * D]), ("B_ada_b", [128, 72]), ("B_norm_w", [128, 24])):
            din(n, shp)
    if do_mix:
        din("h_in", [D, ntok], BF16)
        din("mix_in", [2 * D, ntok], BF16)
        for n in ("w_ro", "w_do", "w_ga", "w_gb", "w_o"):
            din(n, [D, D])
    if do_ffn2:
        din("f2_wgu", [D, 2 * HID])
        din("f2_wd", [HID, D])
    if do_ffn1:
        din("f1_wgu", [D, 2 * HID])
        din("f1_wd", [HID, D])
        dout("h_out", [D, ntok], BF16)
    if do_final:
        din("fnw", [128, 8])
    dout("x_out", [D, ntok])
    xs = nc.dram_tensor("xs_scratch", [D, ntok], F32).ap()

    tiles = tiles_of(LAT_PC, CTX_PC if with_ctx else 0, TT)

    consts = {}
    ones_bf = P.sb("ones_bf", [128, 128], BF16)
    P.memset("pool", ones_bf[:, :], 1.0, [ones_bf])
    eps = P.sb("eps", [128, 1], F32)
    P.memset("pool", eps[:, :], EPS, [eps])
    consts["ones_bf"] = ones_bf
    consts["eps"] = eps
    cc = P.sb("ccol", [128, 8, 2], F32)
    P.dma("sp", cc[:, :, :], dr["ccol"][:, :, :], key=cc, writes=[cc])
    sc = P.sb("sc", [128, 8, 2], F32)
    P.act(sc[:, :, :], cc[:, :, :], AF.Silu, reads=[cc], writes=[sc])
    consts["sc"] = sc
    consts["adaw_st"] = [P.sb("adaw_st", [128, 8, 256], F32) for _ in range(2)]
    consts["modps"] = P.ps("modps", [128, 72, 2])

    mdA = emit_mod(P, dr, "A_", consts, eps) if (do_mix or do_ffn2) else None
    mdB = emit_mod(P, dr, "B_", consts, eps) if do_ffn1 else None

    bufs = {}
    bufs["wbig"] = P.sb("wbig", [128, 8, 2 * HID], BF16)
    bufs["wd"] = P.sb("wd", [128, KH, D], BF16)
    bufs["x"] = [P.sb("xt", [128, 8, TT], F32) for _ in range(2)]
    bufs["h"] = [P.sb("h", [128, 8, TT], BF16) for _ in range(2)]
    bufs["act"] = P.sb("actb", [128, KH, TT], BF16)
    bufs["sq"] = P.sb("sq", [128, 8, TT], BF16)
    bufs["tmp32"] = [P.sb("tmp32", [128, TT], F32) for _ in range(3)]
    bufs["sg"] = [P.sb("sg", [128, TT], F32) for _ in range(2)]
    bufs["ssps"] = P.ps("ssps", [128, TT])
    bufs["gps"] = [P.ps("gps", [128, TT]) for _ in range(2)]
    bufs["ups"] = [P.ps("ups", [128, TT]) for _ in range(2)]
    bufs["yps"] = [P.ps("yps", [128, TT]) for _ in range(2)]
    bufs["st_0"] = Res("st0")
    bufs["st_1"] = Res("st1")

    cur = dr["x_in"]
    passes = []
    if do_mix:
        passes.append("mix")
    if do_ffn2:
        passes.append("ffn2")
    if do_ffn1:
        passes.append("ffn1")
    if do_final and not passes:
        passes.append("finalonly")

    for pi, pname in enumerate(passes):
        last = pi == len(passes) - 1
        dst = dr["x_out"] if last else xs
        if pname == "mix":
            emit_mix_pass(P, dr, consts, mdA, cur, dst, tiles, TT, bufs)
        elif pname == "ffn2":
            post = None
            if do_final and last:
                post = make_final_post(P, dr, consts, bufs)
            emit_ffn_pass(P, dr, consts, mdA, 2, "f2_wgu", "f2_wd", cur, dst, tiles, TT, bufs, post=post)
        elif pname == "ffn1":
            hov = dr["h_out"].rearrange("(k p) t -> p k t", p=128)
            hres = Res("hout_dram")
            h2 = bufs["h"]

            def post(xt, N, s, t0, ti, hov=hov, h2=h2, hres=hres):
                hh = h2[ti % 2]
                emit_norm(P, xt, N, mdB, s, 1, hh, consts, bufs["tmp32"], bufs["ssps"], bufs["sq"])
                P.dma("sp", hov[:, :, t0:t0 + N], hh[:, :, :N], key=hh, reads=[hh], writes=[hres])

            emit_ffn_pass(P, dr, consts, mdB, 0, "f1_wgu", "f1_wd", cur, dst, tiles, TT, bufs, post=post)
            bufs["_final_keys"] = h2
        cur = dst

    fin = [bufs["st_0"], bufs["st_1"]] + bufs.get("_final_keys", [])
    fin = [k for k in fin if id(k) in P.dsem]
    P.wait_all_dma("sp", fin)
    P.emit()
    return nc


def make_final_post(P, dr, consts, bufs):
    fnw = P.sb("fnw", [128, 8], F32)
    P.dma("sp", fnw[:, :], dr["fnw"][:, :], key=fnw, writes=[fnw])

    def post(xt, N, s, t0, ti):
        sq = bufs["sq"]
        ssps = bufs["ssps"]
        rstd = bufs["tmp32"][0]
        P.act(sq[:, :, :N], xt[:, :, :N], AF.Square, reads=[xt], writes=[sq])
        for k in range(8):
            P.mm(ssps[:, :N], consts["ones_bf"][:, :], sq[:, k, :N], k == 0, k == 7,
                 reads=[sq, consts["ones_bf"]], writes=[ssps])
        P.act(rstd[:, :N], ssps[:, :N], AF.Sqrt, reads=[ssps, consts["eps"]], writes=[rstd],
              bias=consts["eps"][:, 0:1], scale=1.0 / D)
        P.op("dve", lambda e: e.reciprocal(rstd[:, :N], rstd[:, :N]), reads=[rstd], writes=[rstd])
        for k in range(8):
            P.stt(xt[:, k, :N], xt[:, k, :N], fnw[:, k:k + 1], rstd[:, :N], ALU.mult, ALU.mult,
                  reads=[xt, fnw, rstd], writes=[xt])
    return post


def emit_mix_pass(P, dr, consts, md, src, dst, tiles, TT, bufs):
    wb = bufs["wbig"]
    wdt = bufs["wd"]
    names = ["w_ro", "w_do", "w_ga", "w_gb", "w_o"]
    for wi, n in enumerate(names):
        v = dr[n].rearrange("(k p) n -> p k n", p=128)
        for k in range(8):
            P.dma("pool", wb[:, k, wi * D:(wi + 1) * D], v[:, k, :], key=wb, writes=[wb])
    W = {n: wi * D for wi, n in enumerate(names)}
    srcv = src.rearrange("(k p) t -> p k t", p=128)
    dstv = dst.rearrange("(k p) t -> p k t", p=128)
    hv = dr["h_in"].rearrange("(k p) t -> p k t", p=128)
    mv = dr["mix_in"].rearrange("(k p) t -> p k t", p=128)
    g = md["g"]
    mres = [wdt.sub(("mix", 0)), wdt.sub(("mix", 1))]
    mkeys = [Res("mk0"), Res("mk1")]

    def mixap(b, k, N):
        return wdt[:, b * 4 + k // 4, (k % 4) * 256:(k % 4) * 256 + N]

    def load(ti):
        t0, N, s = tiles[ti]
        b = ti % 2
        xt = bufs["x"][b]
        h = bufs["h"][b]
        P.dma("sp", xt[:, :, :N], srcv[:, :, t0:t0 + N], key=xt, reads=[dres(bufs, src, ti)], writes=[xt])
        P.dma("sp", h[:, :, :N], hv[:, :, t0:t0 + N], key=h, writes=[h])
        for k4 in range(4):
            P.dma("sp", wdt[:, b * 4 + k4, :].rearrange("p (k t) -> p k t", k=4)[:, :, :N],
                  mv[:, k4 * 4:(k4 + 1) * 4, t0:t0 + N], key=mkeys[b], writes=[mres[b]])

    load(0)
    for ti, (t0, N, s) in enumerate(tiles):
        if ti + 1 < len(tiles):
            load(ti + 1)
        b = ti % 2
        xt = bufs["x"][b]
        h = bufs["h"][b]
        yb = bufs["sq"]
        for d in range(8):
            ps_ya = bufs["gps"][d % 2]
            ps_yb = bufs["ups"][d % 2]
            ps_ga = bufs["yps"][0]
            ps_gb = bufs["yps"][1]
            for k in range(8):
                P.mm(ps_ya[:, :N], wb[:, k, W["w_ro"] + d * 128:W["w_ro"] + (d + 1) * 128], mixap(b, k, N),
                     k == 0, k == 7, reads=[wb, mres[b]], writes=[ps_ya])
            for k in range(8):
                P.mm(ps_yb[:, :N], wb[:, k, W["w_do"] + d * 128:W["w_do"] + (d + 1) * 128], mixap(b, 8 + k, N),
                     k == 0, k == 7, reads=[wb, mres[b]], writes=[ps_yb])
            for k in range(8):
                P.mm(ps_ga[:, :N], wb[:, k, W["w_ga"] + d * 128:W["w_ga"] + (d + 1) * 128], h[:, k, :N],
                     k == 0, k == 7, reads=[wb, h], writes=[ps_ga])
            for k in range(8):
                P.mm(ps_gb[:, :N], wb[:, k, W["w_gb"] + d * 128:W["w_gb"] + (d + 1) * 128], h[:, k, :N],
                     k == 0, k == 7, reads=[wb, h], writes=[ps_gb])
            sa = bufs["sg"][0]
            sb_ = bufs["sg"][1]
            t1 = bufs["tmp32"][1]
            t2 = bufs["tmp32"][2]
            P.act(sa[:, :N], ps_ga[:, :N], AF.Sigmoid, reads=[ps_ga], writes=[sa])
            P.act(sb_[:, :N], ps_gb[:, :N], AF.Sigmoid, reads=[ps_gb], writes=[sb_])
            P.tt("dve", t1[:, :N], sa[:, :N], ps_ya[:, :N], ALU.mult, reads=[sa, ps_ya], writes=[t1])
            P.tt("dve", t2[:, :N], sb_[:, :N], ps_yb[:, :N], ALU.mult, reads=[sb_, ps_yb], writes=[t2])
            P.tt("pool", yb[:, d, :N], t1[:, :N], t2[:, :N], ALU.add, reads=[t1, t2], writes=[yb.sub(d)])
        for d in range(8):
            yp = bufs["yps"][d % 2]
            for k in range(8):
                P.mm(yp[:, :N], wb[:, k, W["w_o"] + d * 128:W["w_o"] + (d + 1) * 128], yb[:, k, :N],
                     k == 0, k == 7, reads=[wb, yb.sub(k)], writes=[yp])
            P.stt(xt[:, d, :N], yp[:, :N], g[:, s, 8 + d:8 + d + 1], xt[:, d, :N], ALU.mult, ALU.add,
                  reads=[yp, g, xt], writes=[xt])
        P.dma("sp", dstv[:, :, t0:t0 + N], xt[:, :, :N], key=bufs["st_" + str(ti % 2)], reads=[xt],
              writes=[dres(bufs, dst, ti)])


class RPool:
    def __init__(self, P, name, shape, dt, n, psum=False):
        self.tiles = [(P.ps if psum else P.sb)(name, shape, dt) for _ in range(n)]
        self.i = 0

    def get(self):
        t = self.tiles[self.i % len(self.tiles)]
        self.i += 1
        return t


NLEV = 7
M_UF, M_UB, M_SF, M_SB, M_LF, M_LB, M_ID, M_ONE = 0, 1, 2, 3, 4, 4 + NLEV, 4 + 2 * NLEV, 5 + 2 * NLEV
NMASK = 6 + 2 * NLEV


def host_masks():
    C = 128
    j = np.arange(C)[:, None]
    i = np.arange(C)[None, :]
    m = np.zeros((C, NMASK, C), np.float32)
    m[:, M_UF] = (j <= i)
    m[:, M_UB] = (j >= i)
    m[:, M_SF] = (i > j)
    m[:, M_SB] = (i < j)
    for k in range(NLEV):
        s = 1 << k
        lf = ((j // s) % 2 == 0) & ((i // s) == (j // s) + 1)
        m[:, M_LF + k] = lf
        m[:, M_LB + k] = lf.T
    m[:, M_ID] = (i == j)
    m[:, M_ONE] = 1.0
    return m


def host_ret_consts(h):
    C = 128
    gam = np.float64(1.0) - np.float64(2.0) ** (-5.0 - h)
    lg = np.log(gam)
    i = np.arange(C, dtype=np.float64)
    rc = np.zeros((C, 3, C), np.float32)
    rc[:, 0, :] = np.exp(lg * (i + 1.0))[None, :]
    rc[:, 1, :] = np.exp(lg * (C - i))[None, :]
    rc[:, 2, :] = np.exp(lg * np.abs(i[:, None] - i[None, :]))
    kfb = np.zeros((C, 4), np.float32)
    kfb[:, 0] = np.exp(lg * (C - 1.0 - i))
    kfb[:, 1] = np.exp(lg * i)
    kfb[:, 2] = np.exp(lg * C)
    return rc, kfb


def build_mixer(NLB=64):
    nc = bass.Bass("TRN2", target_bir_lowering=False)
    P = Prog(nc)
    NT = CTX + NLB * 256
    NCOL = 260 + NLB * 256 + 4
    NCH = NT // 128
    dr = {}

    def din(name, shape, dt=F32):
        dr[name] = nc.dram_tensor(name, list(shape), dt, kind="ExternalInput").ap()

    din("hT", [D, NCOL], BF16)
    din("WF", [D, 1024])
    din("WT", [D, 772])
    din("rope", [128, 2, NT])
    din("convw", [128, 4, 5])
    din("normw", [128, 256])
    din("dnp", [128, 4])
    din("rc", [128, 3, 128])
    din("kfb", [128, 4])
    din("masks", [128, NMASK, 128])
    mix_out = nc.dram_tensor("mix_out", [NT, 512], BF16, kind="ExternalOutput").ap()
    obd = nc.dram_tensor("ob_scratch", [NT, 256], F32).ap()

    WF = P.sb("WF", [128, 8, 1024], BF16)
    WT = P.sb("WT", [128, 8, 772], BF16)
    load_w_bf16(P, WF, dr["WF"], 8, 1024)
    load_w_bf16(P, WT, dr["WT"], 8, 772)
    masks = P.sb("masks", [128, NMASK, 128], F32)
    P.dma("sp", masks[:, :, :], dr["masks"][:, :, :], key=masks, writes=[masks])
    rc = P.sb("rc", [128, 3, 128], F32)
    P.dma("sp", rc[:, :, :], dr["rc"][:, :, :], key=rc, writes=[rc])
    small = {}
    for n, shp in (("convw", [128, 4, 5]), ("normw", [128, 256]), ("dnp", [128, 4]), ("kfb", [128, 4])):
        t = P.sb(n, shp, F32)
        if len(shp) == 3:
            P.dma("sp", t[:, :, :], dr[n][:, :, :], key=t, writes=[t])
        else:
            P.dma("sp", t[:, :], dr[n][:, :], key=t, writes=[t])
        small[n] = t
    convw, normw, dnp, kfb = small["convw"], small["normw"], small["dnp"], small["kfb"]
    negA = P.sb("negA", [128, 2], F32)
    P.act(negA[:, :], dnp[:, 0:2], AF.Exp, reads=[dnp], writes=[negA])
    P.ts("dve", negA[:, :], negA[:, :], -1.0, None, ALU.mult, None, reads=[negA], writes=[negA])
    eps = P.sb("eps", [128, 1], F32)
    P.memset("pool", eps[:, :], EPS, [eps])
    ident = masks[:, M_ID, :]
    ones32 = masks[:, M_ONE, :]
    snap = P.sb("snap", [128, NCH, 256], BF16)
    Sdn = P.sb("Sdn", [128, 256], F32)
    Sr = P.sb("Sr", [128, 256], F32)
    Sr_bf = P.sb("Sr_bf", [128, 256], BF16)
    hwin = [P.sb("hwin", [128, 8, 260], BF16) for _ in range(2)]

    psum = RPool(P, "ps", [128, 512], F32, 8, psum=True)
    f256 = RPool(P, "f256", [128, 256], F32, 14)
    f128 = RPool(P, "f128", [128, 128], F32, 16)
    b256 = RPool(P, "b256", [128, 256], BF16, 8)
    b128 = RPool(P, "b128", [128, 128], BF16, 16)
    colp = RPool(P, "colp", [128, 8], F32, 16)
    gatep = RPool(P, "gatep", [128, 256], F32, 8)
    qgp = RPool(P, "qgp", [128, 128], F32, 4)
    ropep = RPool(P, "ropep", [128, 2, 256], F32, 2)
    outp = RPool(P, "outp", [128, 512], BF16, 4)
    obp = RPool(P, "obp", [128, 256], F32, 4)
    feat32 = RPool(P, "feat32", [128, 256], F32, 12)
    featc = RPool(P, "featc", [128, 128], F32, 12)
    lvl = RPool(P, "lvl", [128, NLEV, 128], F32, 2)
    ob_res = [Res("ob%d" % i) for i in range(NCH)]
    out_keys = []

    blocks = [(0, 0, 0)] + [(260, 256 * i, 256 + 256 * i) for i in range(NLB)]

    def load_h(bi_seq, blk):
        R, t0, tok0 = blk
        hw = hwin[bi_seq % 2]
        P.dma("sp", hw[:, :, :], dr["hT"].rearrange("(k p) t -> p k t", p=128)[:, :, R + t0:R + t0 + 260],
              key=hw, writes=[hw])

    def fm(hw, f, n=260):
        ps = psum.get()
        for k in range(8):
            P.mm(ps[:, :n], WF[:, k, f * 128:(f + 1) * 128], hw[:, k, :n], k == 0, k == 7,
                 reads=[WF, hw], writes=[ps])
        return ps

    def rope(hw, f, rt, scale):
        pa = fm(hw, f)
        pb = fm(hw, f + 1)
        t1 = f256.get()
        t2 = f256.get()
        P.stt(t1[:, :], pa[:, 2:258], scale, rt[:, 0, :], ALU.mult, ALU.mult, reads=[pa, rt], writes=[t1])
        P.stt(t2[:, :], pb[:, 2:258], scale, rt[:, 1, :], ALU.mult, ALU.mult, reads=[pb, rt], writes=[t2])
        o = f256.get()
        P.tt("pool", o[:, :], t1[:, :], t2[:, :], ALU.add, reads=[t1, t2], writes=[o])
        return o

    def features(seq, blk, sweep):
        R, t0, tok0 = blk
        hw = hwin[seq % 2]
        dsel = 0 if sweep == 2 else 1
        F = {"c": [dict(), dict()]}
        rt = ropep.get()
        P.dma("sp", rt[:, :, :], dr["rope"][:, :, tok0:tok0 + 256], key=rt, writes=[rt])
        gall = colp.get()
        ball = colp.get()
        for c in range(2):
            Fc = F["c"][c]
            ps1 = psum.get()
            for k in range(8):
                P.mm(ps1[:, 0:260], hw[:, k, 2 + c * 128:2 + (c + 1) * 128], WT[:, k, 0:260], k == 0, k == 7,
                     reads=[hw, WT], writes=[ps1])
            v_bf = b256.get()
            P.copy("act", v_bf[:, :], ps1[:, 0:256], reads=[ps1], writes=[v_bf])
            Fc["v_bf"] = v_bf
            dba = colp.get()
            P.copy("dve", dba[:, 0:4], ps1[:, 256:260], reads=[ps1], writes=[dba])
            P.act(ball[:, c:c + 1], dba[:, dsel:dsel + 1], AF.Sigmoid, reads=[dba], writes=[ball])
            e1 = colp.get()
            P.act(e1[:, 0:1], dba[:, 2 + dsel:3 + dsel], AF.Exp, reads=[dba, dnp], writes=[e1],
                  bias=dnp[:, 2 + dsel:3 + dsel])
            P.act(e1[:, 1:2], e1[:, 0:1], AF.Ln, reads=[e1], writes=[e1], bias=1.0)
            P.ts("dve", gall[:, c:c + 1], e1[:, 1:2], negA[:, dsel:dsel + 1], None, ALU.mult, None,
                 reads=[e1, negA], writes=[gall])
            if sweep == 2:
                ps2 = psum.get()
                for k in range(8):
                    P.mm(ps2[:, 0:512], hw[:, k, 2 + c * 128:2 + (c + 1) * 128], WT[:, k, 260:772],
                         k == 0, k == 7, reads=[hw, WT], writes=[ps2])
                srg = gatep.get()
                P.act(srg[:, :], ps2[:, 0:256], AF.Silu, reads=[ps2], writes=[srg])
                sdz = f256.get()
                P.act(sdz[:, :], ps2[:, 256:512], AF.Silu, reads=[ps2], writes=[sdz])
                zw = gatep.get()
                P.tt("pool", zw[:, :], sdz[:, :], normw[:, :], ALU.mult, reads=[sdz, normw], writes=[zw])
                Fc["srg"] = srg
                Fc["zw"] = zw
        F["gall"] = gall
        F["ball"] = ball
        kr32 = rope(hw, 2, rt, float(128 ** -0.5))
        for c in range(2):
            Fc = F["c"][c]
            pst = psum.get()
            P.op("pe", lambda e, o=pst[:, 0:128], i=kr32[:, c * 128:(c + 1) * 128]: e.transpose(o, i, ident),
                 reads=[kr32, masks], writes=[pst])
            kd_ = b128.get()
            P.ts("dve", kd_[:, :], pst[:, 0:128], kfb[:, dsel:dsel + 1], None, ALU.mult, None,
                 reads=[pst, kfb], writes=[kd_])
            Fc["kdec"] = kd_
        if sweep == 2:
            kT_bf = b256.get()
            P.copy("act", kT_bf[:, :], kr32[:, :], reads=[kr32], writes=[kT_bf])
            qr32 = rope(hw, 0, rt, 1.0)
            qT_bf = b256.get()
            P.copy("act", qT_bf[:, :], qr32[:, :], reads=[qr32], writes=[qT_bf])
            F["kT_bf"] = kT_bf
            F["qT_bf"] = qT_bf
            for c in range(2):
                qdf = b128.get()
                qdb = b128.get()
                P.tt("pool", qdf[:, :], qr32[:, c * 128:(c + 1) * 128], rc[:, 0, :], ALU.mult,
                     reads=[qr32, rc], writes=[qdf])
                P.tt("pool", qdb[:, :], qr32[:, c * 128:(c + 1) * 128], rc[:, 1, :], ALU.mult,
                     reads=[qr32, rc], writes=[qdb])
                F["c"][c]["qdf"] = qdf
                F["c"][c]["qdb"] = qdb
        sl = []
        for fi in range(4):
            ps = fm(hw, 4 + fi)
            cv = f256.get()
            P.ts("dve", cv[:, :], ps[:, 0:256], convw[:, fi, 0:1], None, ALU.mult, None,
                 reads=[ps, convw], writes=[cv])
            for j in range(1, 5):
                P.stt(cv[:, :], ps[:, j:j + 256], convw[:, fi, j:j + 1], cv[:, :], ALU.mult, ALU.add,
                      reads=[ps, convw, cv], writes=[cv])
            s_ = f256.get() if fi < 2 else feat32.get()
            P.act(s_[:, :], cv[:, :], AF.Silu, reads=[cv], writes=[s_])
            sl.append(s_)
        nrm = []
        for fi in range(2):
            sq = f256.get()
            P.act(sq[:, :], sl[fi][:, :], AF.Square, reads=[sl[fi]], writes=[sq])
            ssp = psum.get()
            P.mm(ssp[:, 0:256], ones32, sq[:, :], True, True, reads=[masks, sq], writes=[ssp])
            rn = f256.get()
            P.act(rn[:, :], ssp[:, 0:256], AF.Sqrt, reads=[ssp, eps], writes=[rn], bias=eps[:, 0:1])
            P.op("dve", lambda e, r=rn: e.reciprocal(r[:, :], r[:, :]), reads=[rn], writes=[rn])
            o = feat32.get()
            P.stt(o[:, :], sl[fi][:, :], float(128 ** -0.5) if fi == 0 else 1.0, rn[:, :], ALU.mult, ALU.mult,
                  reads=[sl[fi], rn], writes=[o])
            nrm.append(o)
        F["qT32"] = nrm[0]
        F["kT32"] = nrm[1]
        for c in range(2):
            Fc = F["c"][c]
            pst = psum.get()
            P.op("pe", lambda e, o=pst[:, 0:128], i=nrm[1][:, c * 128:(c + 1) * 128]: e.transpose(o, i, ident),
                 reads=[nrm[1], masks], writes=[pst])
            ktm = featc.get()
            P.copy("act", ktm[:, :], pst[:, 0:128], reads=[pst], writes=[ktm])
            Fc["ktm"] = ktm
            psv = psum.get()
            for e2 in range(2):
                P.op("pe", lambda e, o=psv[:, e2 * 128:(e2 + 1) * 128],
                     i=sl[2 + e2][:, c * 128:(c + 1) * 128]: e.transpose(o, i, ident),
                     reads=[sl[2 + e2], masks], writes=[psv])
            vb = feat32.get()
            P.ts("dve", vb[:, :], psv[:, 0:256], ball[:, c:c + 1], None, ALU.mult, None,
                 reads=[psv, ball], writes=[vb])
            Fc["vb"] = vb
        mu = M_UF if dsel == 0 else M_UB
        psg = psum.get()
        P.mm(psg[:, 0:2], masks[:, mu, :], gall[:, 0:2], True, True, reads=[masks, gall], writes=[psg])
        P.mm(psg[:, 2:4], ones32, gall[:, 0:2], True, True, reads=[masks, gall], writes=[psg])
        cols = colp.get()
        P.copy("dve", cols[:, 0:4], psg[:, 0:4], reads=[psg], writes=[cols])
        ex = colp.get()
        P.act(ex[:, 0:4], cols[:, 0:4], AF.Exp, reads=[cols], writes=[ex])
        P.tt("dve", cols[:, 4:6], cols[:, 2:4], cols[:, 0:2], ALU.subtract, reads=[cols], writes=[cols])
        P.act(ex[:, 4:6], cols[:, 4:6], AF.Exp, reads=[cols], writes=[ex])
        P.tt("dve", ex[:, 6:8], ex[:, 0:2], ball[:, 0:2], ALU.mult, reads=[ex, ball], writes=[ex])
        F["cols"] = cols
        F["ex"] = ex
        return F

    def delta_chunk(F, c, dsel):
        Fc = F["c"][c]
        cols, ex, ball = F["cols"], F["ex"], F["ball"]
        kT = F["kT32"][:, c * 128:(c + 1) * 128]
        qT = F["qT32"][:, c * 128:(c + 1) * 128]
        mincl = masks[:, M_UF if dsel == 0 else M_UB, :]
        mstr = masks[:, M_SF if dsel == 0 else M_SB, :]
        dg = f256.get()
        P.ts("pool", dg[:, 0:128], ident, cols[:, c:c + 1], None, ALU.mult, None, reads=[masks, cols], writes=[dg])
        P.ts("pool", dg[:, 128:256], ident, ball[:, c:c + 1], None, ALU.mult, None, reads=[masks, ball], writes=[dg])
        prow = psum.get()
        P.mm(prow[:, 0:256], ones32, dg[:, :], True, True, reads=[masks, dg], writes=[prow])
        gd = f128.get()
        P.ts("dve", gd[:, :], prow[:, 0:128], cols[:, c:c + 1], 0.0, ALU.subtract, ALU.min,
             reads=[prow, cols], writes=[gd])
        dec = f128.get()
        P.act(dec[:, :], gd[:, :], AF.Exp, reads=[gd], writes=[dec])
        decm = f128.get()
        P.tt("pool", decm[:, :], dec[:, :], mincl, ALU.mult, reads=[dec, masks], writes=[decm])
        egrow = f128.get()
        P.act(egrow[:, :], prow[:, 0:128], AF.Exp, reads=[prow], writes=[egrow])
        qg = qgp.get()
        P.tt("pool", qg[:, :], qT, egrow[:, :], ALU.mult, reads=[F["qT32"], egrow], writes=[qg])
        bm = f128.get()
        P.tt("dve", bm[:, :], prow[:, 128:256], mstr, ALU.mult, reads=[prow, masks], writes=[bm])
        pkk = psum.get()
        P.mm(pkk[:, 0:128], kT, kT, True, True, reads=[F["kT32"]], writes=[pkk])
        P.mm(pkk[:, 128:256], kT, qT, True, True, reads=[F["kT32"], F["qT32"]], writes=[pkk])
        t1 = f128.get()
        P.tt("dve", t1[:, :], pkk[:, 0:128], decm[:, :], ALU.mult, reads=[pkk, decm], writes=[t1])
        Bm = f128.get()
        P.tt("pool", Bm[:, :], t1[:, :], bm[:, :], ALU.mult, reads=[t1, bm], writes=[Bm])
        attnT = b128.get()
        P.tt("dve", attnT[:, :], pkk[:, 128:256], decm[:, :], ALU.mult, reads=[pkk, decm], writes=[attnT])
        Bk = lvl.get()
        ml = M_LF if dsel == 0 else M_LB
        for k in range(NLEV):
            P.tt("pool", Bk[:, k, :], Bm[:, :], masks[:, ml + k, :], ALU.mult, reads=[Bm, masks], writes=[Bk.sub(k)])
        Dm = f128.get()
        Dt = f128.get()
        pa0 = psum.get()
        P.op("pe", lambda e, o=pa0[:, 0:128], i=Bk[:, 0, :]: e.transpose(o, i, ident), reads=[Bk.sub(0), masks],
             writes=[pa0])
        P.tt("dve", Dm[:, :], ident, pa0[:, 0:128], ALU.subtract, reads=[masks, pa0], writes=[Dm])
        P.tt("pool", Dt[:, :], ident, Bk[:, 0, :], ALU.subtract, reads=[masks, Bk.sub(0)], writes=[Dt])
        for k in range(1, NLEV):
            lastl = k == NLEV - 1
            py = psum.get()
            P.mm(py[:, 0:128], Bk[:, k, :], Dm[:, :], True, True, reads=[Bk.sub(k), Dm], writes=[py])
            Y = f128.get()
            P.copy("act", Y[:, :], py[:, 0:128], reads=[py], writes=[Y])
            pz = psum.get()
            if not lastl:
                P.mm(pz[:, 0:128], Dt[:, :], Y[:, :], True, True, reads=[Dt, Y], writes=[pz])
            P.mm(pz[:, 128:256], Y[:, :], Dt[:, :], True, True, reads=[Y, Dt], writes=[pz])
            if not lastl:
                Dn = f128.get()
                P.tt("dve", Dn[:, :], Dm[:, :], pz[:, 0:128], ALU.subtract, reads=[Dm, pz], writes=[Dn])
                Dm = Dn
            Dtn = f128.get()
            P.tt("dve", Dtn[:, :], Dt[:, :], pz[:, 128:256], ALU.subtract, reads=[Dt, pz], writes=[Dtn])
            Dt = Dtn
        Nn = Dt
        kbg = f128.get()
        P.ts("pool", kbg[:, :], Fc["ktm"][:, :], ex[:, 6 + c:7 + c], None, ALU.mult, None,
             reads=[Fc["ktm"], ex], writes=[kbg])
        kd = b128.get()
        P.ts("pool", kd[:, :], Fc["ktm"][:, :], ex[:, 4 + c:5 + c], None, ALU.mult, None,
             reads=[Fc["ktm"], ex], writes=[kd])
        pw = psum.get()
        P.mm(pw[:, 0:128], kbg[:, :], Nn[:, :], True, True, reads=[kbg, Nn], writes=[pw])
        nwT = f128.get()
        P.act(nwT[:, :], pw[:, 0:128], AF.Copy, reads=[pw], writes=[nwT], scale=-1.0)
        pv = psum.get()
        P.mm(pv[:, 0:256], Nn[:, :], Fc["vb"][:, :], True, False, reads=[Nn, Fc["vb"]], writes=[pv])
        P.mm(pv[:, 0:256], nwT[:, :], Sdn[:, :], False, True, reads=[nwT, Sdn], writes=[pv])
        vn = b256.get()
        P.copy("act", vn[:, :], pv[:, 0:256], reads=[pv], writes=[vn])
        po = psum.get()
        P.mm(po[:, 0:256], qg[:, :], Sdn[:, :], True, False, reads=[qg, Sdn], writes=[po])
        P.mm(po[:, 0:256], attnT[:, :], vn[:, :], False, True, reads=[attnT, vn], writes=[po])
        pds = psum.get()
        P.mm(pds[:, 0:256], kd[:, :], vn[:, :], True, True, reads=[kd, vn], writes=[pds])
        P.stt(Sdn[:, :], Sdn[:, :], ex[:, 2 + c:3 + c], pds[:, 0:256], ALU.mult, ALU.add,
              reads=[Sdn, ex, pds], writes=[Sdn])
        return po

    def rms_gate(po_ap, po_res, gate, outap, outres, extra=None):
        junk = f256.get()
        ss = colp.get()
        P.op("act", lambda e: e.activation(junk[:, :], po_ap, AF.Square, accum_out=ss[:, 0:1]),
             reads=[po_res], writes=[junk, ss])
        P.act(ss[:, 1:2], ss[:, 0:1], AF.Sqrt, reads=[ss, eps], writes=[ss], bias=eps[:, 0:1], scale=1.0 / 256)
        P.op("dve", lambda e: e.reciprocal(ss[:, 2:3], ss[:, 1:2]), reads=[ss], writes=[ss])
        P.stt(outap, po_ap, ss[:, 2:3], gate[:, :], ALU.mult, ALU.mult, reads=[po_res, ss, gate], writes=[outres])

    P.memset("pool", Sdn[:, :], 0.0, [Sdn])
    P.memset("pool", Sr[:, :], 0.0, [Sr])
    order1 = [blocks[0]] + blocks[1:][::-1]
    load_h(0, order1[0])
    for seq, blk in enumerate(order1):
        if seq + 1 < len(order1):
            load_h(seq + 1, order1[seq + 1])
        R, t0, tok0 = blk
        F = features(seq, blk, 1)
        for c in (1, 0):
            gci = tok0 // 128 + c
            Fc = F["c"][c]
            P.copy("act", snap[:, gci, :], Sr[:, :], reads=[Sr], writes=[snap.sub(gci)])
            pkv = psum.get()
            P.mm(pkv[:, 0:256], Fc["kdec"][:, :], Fc["v_bf"][:, :], True, True, reads=[Fc["kdec"], Fc["v_bf"]],
                 writes=[pkv])
            P.stt(Sr[:, :], Sr[:, :], kfb[:, 2:3], pkv[:, 0:256], ALU.mult, ALU.add, reads=[Sr, kfb, pkv],
                  writes=[Sr])
            po = delta_chunk(F, c, 1)
            ob = obp.get()
            P.copy("act", ob[:, :], po[:, 0:256], reads=[po], writes=[ob])
            P.dma("sp", obd[gci * 128:(gci + 1) * 128, :], ob[:, :], key=ob, reads=[ob], writes=[ob_res[gci]])

    P.memset("pool", Sdn[:, :], 0.0, [Sdn])
    P.memset("pool", Sr[:, :], 0.0, [Sr])
    P.memset("pool", Sr_bf[:, :], 0.0, [Sr_bf])
    base = len(order1)
    load_h(base, blocks[0])
    for seq0, blk in enumerate(blocks):
        seq = base + seq0
        if seq0 + 1 < len(blocks):
            load_h(seq + 1, blocks[seq0 + 1])
        R, t0, tok0 = blk
        F = features(seq, blk, 2)
        for c in (0, 1):
            gci = tok0 // 128 + c
            Fc = F["c"][c]
            outb = outp.get()
            psc = psum.get()
            P.mm(psc[:, 0:128], F["kT_bf"][:, c * 128:(c + 1) * 128], F["qT_bf"][:, c * 128:(c + 1) * 128],
                 True, True, reads=[F["kT_bf"], F["qT_bf"]], writes=[psc])
            scm = b128.get()
            P.tt("dve", scm[:, :], psc[:, 0:128], rc[:, 2, :], ALU.mult, reads=[psc, rc], writes=[scm])
            pro = psum.get()
            P.mm(pro[:, 0:256], scm[:, :], Fc["v_bf"][:, :], True, False, reads=[scm, Fc["v_bf"]], writes=[pro])
            P.mm(pro[:, 0:256], Fc["qdf"][:, :], Sr_bf[:, :], False, False, reads=[Fc["qdf"], Sr_bf], writes=[pro])
            P.mm(pro[:, 0:256], Fc["qdb"][:, :], snap[:, gci, :], False, True, reads=[Fc["qdb"], snap.sub(gci)],
                 writes=[pro])
            rms_gate(pro[:, 0:256], pro, Fc["srg"], outb[:, 0:256], outb)
            pkv = psum.get()
            P.mm(pkv[:, 0:256], Fc["kdec"][:, :], Fc["v_bf"][:, :], True, True, reads=[Fc["kdec"], Fc["v_bf"]],
                 writes=[pkv])
            P.stt(Sr[:, :], Sr[:, :], kfb[:, 2:3], pkv[:, 0:256], ALU.mult, ALU.add, reads=[Sr, kfb, pkv],
                  writes=[Sr])
            P.copy("act", Sr_bf[:, :], Sr[:, :], reads=[Sr], writes=[Sr_bf])
            ob = obp.get()
            P.dma("sp", ob[:, :], obd[gci * 128:(gci + 1) * 128, :], key=ob, reads=[ob_res[gci]], writes=[ob])
            po = delta_chunk(F, c, 0)
            osum = f256.get()
            P.tt("dve", osum[:, :], po[:, 0:256], ob[:, :], ALU.add, reads=[po, ob], writes=[osum])
            rms_gate(osum[:, :], osum, Fc["zw"], outb[:, 256:512], outb)
            P.dma("sp", mix_out[gci * 128:(gci + 1) * 128, :], outb[:, :], key=outb, reads=[outb], writes=[])
            if outb not in out_keys:
                out_keys.append(outb)
    P.wait_all_dma("sp", out_keys)
    P.emit()
    return nc


def host_rope_tables(nlat):
    half = 64
    n_freq = 32
    inv = (10000.0 ** (-np.arange(n_freq, dtype=np.float32) / n_freq)).astype(np.float32)
    rows = nlat // 64
    ang_r = np.arange(rows, dtype=np.float32)[:, None] * inv
    ang_c = np.arange(64, dtype=np.float32)[:, None] * inv
    ang = np.concatenate([np.broadcast_to(ang_r[:, None, :], (rows, 64, n_freq)),
                          np.broadcast_to(ang_c[None, :, :], (rows, 64, n_freq))], axis=-1).reshape(rows * 64, half)
    cos = np.cos(ang).astype(np.float32)
    sin = np.sin(ang).astype(np.float32)
    t = np.zeros((128, 2, CTX + nlat), np.float32)
    t[:, 0, :CTX] = 1.0
    t[:64, 0, CTX:] = cos.T
    t[64:, 0, CTX:] = cos.T
    t[:64, 1, CTX:] = -sin.T
    t[64:, 1, CTX:] = sin.T
    return t


_CONST_CACHE = {}


def mixer_inputs(hc, hx, inp, lay, h, NLB):
    import ml_dtypes
    L = NLB * 256
    NCOL = 260 + L + 4
    hT = np.zeros((D, NCOL), ml_dtypes.bfloat16)
    hT[:, 2:258] = np.asarray(hc).T.astype(ml_dtypes.bfloat16) if hc.dtype != ml_dtypes.bfloat16 else np.asarray(hc).T
    hT[:, 262:262 + L] = np.asarray(hx).T.astype(ml_dtypes.bfloat16) if hx.dtype != ml_dtypes.bfloat16 else np.asarray(hx).T
    w = inp["w_in"][lay]
    off = np.cumsum([0, 512, 512, 1024, 1024, 512, 512, 1024, 1024, 16, 1024, 1024])
    rq = w[:, off[0] + h * 128: off[0] + (h + 1) * 128]
    rk = w[:, off[1] + h * 128: off[1] + (h + 1) * 128]
    rv = w[:, off[2] + h * 256: off[2] + (h + 1) * 256]
    rg = w[:, off[3] + h * 256: off[3] + (h + 1) * 256]
    dq = w[:, off[4] + h * 128: off[4] + (h + 1) * 128]
    dk = w[:, off[5] + h * 128: off[5] + (h + 1) * 128]
    dv = w[:, off[6] + h * 256: off[6] + (h + 1) * 256]
    dz = w[:, off[7] + h * 256: off[7] + (h + 1) * 256]
    dba = w[:, off[8]: off[8] + 16][:, [h, 4 + h, 8 + h, 12 + h]]
    sw = lambda m: np.concatenate([m[:, 64:], m[:, :64]], axis=1)
    WF = np.ascontiguousarray(np.concatenate([rq, sw(rq), rk, sw(rk), dq, dk, dv], axis=1))
    WT = np.ascontiguousarray(np.concatenate([rv, dba, rg, dz], axis=1))
    cw = inp["dn_conv_w"][lay]
    chans = [cw[:, h * 128:(h + 1) * 128], cw[:, 512 + h * 128:512 + (h + 1) * 128],
             cw[:, 1024 + h * 256:1024 + h * 256 + 128], cw[:, 1024 + h * 256 + 128:1024 + (h + 1) * 256]]
    convw = np.ascontiguousarray(np.stack([c.T for c in chans], axis=1))
    normw = np.ascontiguousarray(np.broadcast_to(inp["dn_norm_w"][lay][None, :], (128, 256)))
    dnp = np.zeros((128, 4), np.float32)
    dnp[:, 0] = inp["dn_a_log"][lay][0, h]
    dnp[:, 1] = inp["dn_a_log"][lay][1, h]
    dnp[:, 2] = inp["dn_dt_bias"][lay][0, h]
    dnp[:, 3] = inp["dn_dt_bias"][lay][1, h]
    if ("rope", NLB) not in _CONST_CACHE:
        _CONST_CACHE[("rope", NLB)] = host_rope_tables(L)
        _CONST_CACHE["masks"] = host_masks()
    if ("rc", h) not in _CONST_CACHE:
        _CONST_CACHE[("rc", h)] = host_ret_consts(h)
    rc, kfb = _CONST_CACHE[("rc", h)]
    return {"hT": hT, "WF": WF, "WT": WT, "rope": _CONST_CACHE[("rope", NLB)], "convw": convw, "normw": normw,
            "dnp": dnp, "rc": rc, "kfb": kfb, "masks": _CONST_CACHE["masks"]}


_PROGS = {}
DEBUG = {}


def _prog(key):
    if key not in _PROGS:
        if key[0] == "row":
            _PROGS[key] = build_rowlocal(*key[1:])
        else:
            _PROGS[key] = build_mixer(NLB=key[1])
    return _PROGS[key]


def _run(nc, in_maps):
    res = run_bass_kernel_spmd(nc, in_maps, core_ids=list(range(NCORES)))
    return res.results


def _layer_small(inp, lay, pref):
    return {pref + "ada_w": np.ascontiguousarray(inp["ada_w"][lay]),
            pref + "ada_b": col_layout(inp["ada_b"][lay]),
            pref + "norm_w": col_layout(inp["norm_w"][lay].reshape(-1))}


def kernel(**inp):
    import ml_dtypes
    inp = {k: np.asarray(v) for k, v in inp.items()}
    NLB = LAT // 256
    depth = inp["ada_w"].shape[0]
    ccols = [np.ascontiguousarray(np.stack([col_layout(inp["c"][b]), col_layout(inp["c_ctx"])], -1)) for b in range(2)]
    in_maps = []
    for core in range(NCORES):
        b, q = core // 4, core % 4
        xT = np.ascontiguousarray(np.concatenate([inp["x"][b, q * LAT_PC:(q + 1) * LAT_PC],
                                                  inp["ctx"][b, q * CTX_PC:(q + 1) * CTX_PC]], 0).T)
        m = {"x_in": xT, "ccol": ccols[b], "f1_wgu": np.ascontiguousarray(inp["ffn1_wgu"][0]),
             "f1_wd": np.ascontiguousarray(inp["ffn1_wd"][0])}
        m.update(_layer_small(inp, 0, "B_"))
        in_maps.append(m)
    res = _run(_prog(("row", False, False, True, False)), in_maps)
    xcur = [r["x_out"] for r in res]
    hcur = [np.asarray(r["h_out"]) for r in res]
    for lay in range(depth):
        last = lay == depth - 1
        hT_b = []
        for b in range(2):
            hT = np.zeros((D, 260 + LAT + 4), ml_dtypes.bfloat16)
            for q in range(4):
                hh = hcur[b * 4 + q]
                hT[:, 2 + q * CTX_PC:2 + (q + 1) * CTX_PC] = hh[:, LAT_PC:]
                hT[:, 262 + q * LAT_PC:262 + (q + 1) * LAT_PC] = hh[:, :LAT_PC]
            hT_b.append(hT)
        in_maps = []
        for core in range(NCORES):
            b, h = core // 4, core % 4
            m = _mixer_inputs_fast(hT_b[b], inp, lay, h, NLB)
            in_maps.append(m)
        res = _run(_prog(("mix", NLB)), in_maps)
        mo = [np.asarray(r["mix_out"]) for r in res]
        if DEBUG is not None and "keep" in DEBUG:
            DEBUG["mo%d" % lay] = mo
        in_maps = []
        for core in range(NCORES):
            b, q = core // 4, core % 4
            mix = np.zeros((2 * D, TOK), ml_dtypes.bfloat16)
            for h in range(4):
                o = mo[b * 4 + h]
                lat = o[CTX + q * LAT_PC:CTX + (q + 1) * LAT_PC]
                cx = o[q * CTX_PC:(q + 1) * CTX_PC]
                both = np.concatenate([lat, cx], 0)
                mix[h * 256:(h + 1) * 256] = both[:, :256].T
                mix[D + h * 256:D + (h + 1) * 256] = both[:, 256:].T
            w_in = inp["w_in"][lay]
            m = {"x_in": xcur[core], "ccol": ccols[b], "h_in": hcur[core], "mix_in": mix,
                 "w_ro": np.ascontiguousarray(inp["w_ret_out"][lay]), "w_do": np.ascontiguousarray(inp["w_dn_out"][lay]),
                 "w_ga": np.ascontiguousarray(w_in[:, 6160:7184]), "w_gb": np.ascontiguousarray(w_in[:, 7184:8208]),
                 "w_o": np.ascontiguousarray(inp["w_o"][lay]),
                 "f2_wgu": np.ascontiguousarray(inp["ffn2_wgu"][lay]), "f2_wd": np.ascontiguousarray(inp["ffn2_wd"][lay])}
            m.update(_layer_small(inp, lay, "A_"))
            if not last:
                m.update(_layer_small(inp, lay + 1, "B_"))
                m["f1_wgu"] = np.ascontiguousarray(inp["ffn1_wgu"][lay + 1])
                m["f1_wd"] = np.ascontiguousarray(inp["ffn1_wd"][lay + 1])
            else:
                m["fnw"] = col_layout(inp["final_norm_w"])
            in_maps.append(m)
        res = _run(_prog(("row", True, True, not last, last)), in_maps)
        xcur = [r["x_out"] for r in res]
        if not last:
            hcur = [np.asarray(r["h_out"]) for r in res]
        if DEBUG is not None and "keep" in DEBUG:
            DEBUG["x%d" % lay] = xcur
    out = np.zeros((2, LAT, D), np.float32)
    for core in range(NCORES):
        b, q = core // 4, core % 4
        out[b, q * LAT_PC:(q + 1) * LAT_PC] = xcur[core][:, :LAT_PC].T
    return out


def _mixer_inputs_fast(hT, inp, lay, h, NLB):
    mm_ = _mixer_weights(inp, lay, h, NLB)
    mm_ = dict(mm_)
    mm_["hT"] = hT
    return mm_


def _mixer_weights(inp, lay, h, NLB):
    import ml_dtypes
    hc = np.zeros((256, D), ml_dtypes.bfloat16)
    hx = np.zeros((NLB * 256, D), ml_dtypes.bfloat16)
    m = mixer_inputs(hc, hx, inp, lay, h, NLB)
    return m
```
